# Optimizing a Trainium2 kernel written in Bass

```python
import jax, jax.numpy as jnp
from jax import lax
import numpy as np

D_MODEL = 2048
BATCH = 8
SEQ = 2048
DEPTH = 1

LRU_WIDTH = 2560
LRU_HEADS = 20
LRU_HEAD_DIM = LRU_WIDTH // LRU_HEADS
LRU_CONV = 4
LRU_CONV_LEFT = 2
LRU_C = 8.0
CONV_WIDTH = 2048
CONV_GROUPS = 16
SHORT_CONV = 3
SHORT_CONV_LEFT = 1
D_FF = 4 * D_MODEL
ALPHA = float((2 * DEPTH) ** 0.25)
BETA = float((8 * DEPTH) ** -0.25)
LN_EPS = 1e-5
SPLIT_SIZES = (LRU_WIDTH, CONV_WIDTH, CONV_WIDTH, CONV_WIDTH, D_MODEL, D_MODEL)
SPLIT_POINTS = tuple(int(v) for v in np.cumsum(SPLIT_SIZES)[:-1])
IN_COLS = int(sum(SPLIT_SIZES))

kernel_name = "hybrid_rglru_shortconv_deepnorm_block"


def layer_norm(x, g, b):
    xf = x.astype(jnp.float32)
    mu = jnp.mean(xf, axis=-1, keepdims=True)
    var = jnp.mean(jnp.square(xf - mu), axis=-1, keepdims=True)
    y = (xf - mu) * lax.rsqrt(var + LN_EPS)
    return (y * g.astype(jnp.float32) + b.astype(jnp.float32)).astype(x.dtype)


def centred_dwconv(u, w, b, left):
    k_w = w.shape[0]
    s = u.shape[1]
    up = jnp.pad(u, ((0, 0), (left, k_w - 1 - left), (0, 0)))
    return sum(up[:, k:k + s] * w[k] for k in range(k_w)) + b


def _lin_combine(left, right):
    a1, b1 = left
    a2, b2 = right
    return a1 * a2, a2 * b1 + b2


def rg_lru_direction(u, w_a, b_a, w_x, b_x, lam, reverse):
    bsz, s, _ = u.shape
    uh = u.reshape(bsz, s, LRU_HEADS, LRU_HEAD_DIM)
    r = jax.nn.sigmoid(jnp.einsum('bshd,hde->bshe', uh, w_a) + b_a).reshape(bsz, s, LRU_WIDTH)
    i = jax.nn.sigmoid(jnp.einsum('bshd,hde->bshe', uh, w_x) + b_x).reshape(bsz, s, LRU_WIDTH)
    log_a = -LRU_C * r.astype(jnp.float32) * jax.nn.softplus(-lam.astype(jnp.float32))
    a = jnp.exp(log_a)
    gated_x = jnp.sqrt(-jnp.expm1(2.0 * log_a)) * (i * u).astype(jnp.float32)
    _, h = lax.associative_scan(_lin_combine, (a, gated_x), reverse=reverse, axis=1)
    return h.astype(u.dtype)


def setup_inputs(seed: int = 0) -> dict:
    key = jax.random.key(seed)
    ks = jax.random.split(key, 24)
    f32 = jnp.float32
    nrm = lambda k, shape, scale: jax.random.normal(k, shape, f32) * scale
    x = jax.random.normal(ks[0], (BATCH, SEQ, D_MODEL), f32)
    w_in = nrm(ks[1], (D_MODEL, IN_COLS), D_MODEL ** -0.5)
    lru_conv_w = nrm(ks[2], (LRU_CONV, LRU_WIDTH), LRU_CONV ** -0.5)
    lru_conv_b = nrm(ks[3], (LRU_WIDTH,), 0.01)
    lru_w_a = nrm(ks[4], (2, LRU_HEADS, LRU_HEAD_DIM, LRU_HEAD_DIM), LRU_HEAD_DIM ** -0.5)
    lru_b_a = nrm(ks[5], (2, LRU_HEADS, LRU_HEAD_DIM), 0.01)
    lru_w_x = nrm(ks[6], (2, LRU_HEADS, LRU_HEAD_DIM, LRU_HEAD_DIM), LRU_HEAD_DIM ** -0.5)
    lru_b_x = nrm(ks[7], (2, LRU_HEADS, LRU_HEAD_DIM), 0.01)
    a_c = jax.random.uniform(ks[8], (2, LRU_WIDTH), f32, 0.9, 0.999)
    a0 = a_c ** (1.0 / LRU_C)
    lru_lambda = jnp.log(a0) - jnp.log1p(-a0)
    w_lru_out = nrm(ks[9], (LRU_WIDTH, D_MODEL), BETA * LRU_WIDTH ** -0.5)
    sc_conv_w = nrm(ks[10], (SHORT_CONV, CONV_WIDTH), SHORT_CONV ** -0.5)
    sc_conv_b = nrm(ks[11], (CONV_WIDTH,), 0.01)
    w_conv_out = nrm(ks[12], (CONV_WIDTH, D_MODEL), BETA * CONV_WIDTH ** -0.5)
    w_o = nrm(ks[13], (D_MODEL, D_MODEL), BETA * D_MODEL ** -0.5)
    ln1_g = 1.0 + nrm(ks[14], (D_MODEL,), 0.01)
    ln1_b = nrm(ks[15], (D_MODEL,), 0.01)
    mlp_w1 = nrm(ks[16], (D_MODEL, D_FF), D_MODEL ** -0.5)
    mlp_b1 = nrm(ks[17], (D_FF,), 0.01)
    mlp_w2 = nrm(ks[18], (D_FF, D_MODEL), BETA * D_FF ** -0.5)
    mlp_b2 = nrm(ks[19], (D_MODEL,), 0.01)
    ln2_g = 1.0 + nrm(ks[20], (D_MODEL,), 0.01)
    ln2_b = nrm(ks[21], (D_MODEL,), 0.01)
    return {"x": x, "w_in": w_in, "lru_conv_w": lru_conv_w, "lru_conv_b": lru_conv_b,
            "lru_w_a": lru_w_a, "lru_b_a": lru_b_a, "lru_w_x": lru_w_x, "lru_b_x": lru_b_x,
            "lru_lambda": lru_lambda, "w_lru_out": w_lru_out, "sc_conv_w": sc_conv_w,
            "sc_conv_b": sc_conv_b, "w_conv_out": w_conv_out, "w_o": w_o,
            "ln1_g": ln1_g, "ln1_b": ln1_b, "mlp_w1": mlp_w1, "mlp_b1": mlp_b1,
            "mlp_w2": mlp_w2, "mlp_b2": mlp_b2, "ln2_g": ln2_g, "ln2_b": ln2_b}


def reference(x, w_in, lru_conv_w, lru_conv_b, lru_w_a, lru_b_a, lru_w_x, lru_b_x,
              lru_lambda, w_lru_out, sc_conv_w, sc_conv_b, w_conv_out, w_o,
              ln1_g, ln1_b, mlp_w1, mlp_b1, mlp_w2, mlp_b2, ln2_g, ln2_b):
    h = x
    for _ in range(DEPTH):
        proj = jnp.einsum('bsd,dc->bsc', h, w_in)
        lru_x, conv_x, conv_bg, conv_cg, g_lru, g_conv = jnp.split(proj, SPLIT_POINTS, axis=-1)
        u = centred_dwconv(lru_x, lru_conv_w, lru_conv_b, LRU_CONV_LEFT)
        h_fwd = rg_lru_direction(u, lru_w_a[0], lru_b_a[0], lru_w_x[0], lru_b_x[0], lru_lambda[0], False)
        h_bwd = rg_lru_direction(u, lru_w_a[1], lru_b_a[1], lru_w_x[1], lru_b_x[1], lru_lambda[1], True)
        y_lru = jnp.einsum('bsw,wd->bsd', h_fwd + h_bwd, w_lru_out)
        v = centred_dwconv(conv_cg * conv_x, sc_conv_w, sc_conv_b, SHORT_CONV_LEFT)
        y_conv = jnp.einsum('bsc,cd->bsd', conv_bg * v, w_conv_out)
        merged = jax.nn.sigmoid(g_lru) * y_lru + jax.nn.sigmoid(g_conv) * y_conv
        mix_out = jnp.einsum('bsd,de->bse', merged, w_o)
        h = layer_norm(ALPHA * h + mix_out, ln1_g, ln1_b)
        ff = jnp.square(jax.nn.relu(jnp.einsum('bsd,df->bsf', h, mlp_w1) + mlp_b1))
        ff_out = jnp.einsum('bsf,fd->bsd', ff, mlp_w2) + mlp_b2
        h = layer_norm(ALPHA * h + ff_out, ln2_g, ln2_b)
    return h
```

```python
import numpy as np
import concourse.bass as bass
import concourse.mybir as mybir
from concourse.bass_utils import run_bass_kernel_spmd

F32 = mybir.dt.float32
BF16 = mybir.dt.bfloat16
AF = mybir.ActivationFunctionType
ALU = mybir.AluOpType

D = 2048
S = 2048
LW = 2560
H = 20
CW = 2048
DFF = 8192
KC = D // 128
NT = 512
NTILES = S // NT
NCH = S // 512
ALPHA = float(2.0 ** 0.25)
LN_EPS = 1e-5
FG = 8
NFG = (DFF // 128) // FG

_c = 0
def _col(n):
    global _c
    r = _c
    _c += n
    return r
LCW = _col(H * 4)
LCB = _col(H)
BA = _col(2 * H)
BX = _col(2 * H)
LAM = _col(2 * H)
SCW = _col(16 * 3)
SCB = _col(16)
LN1G = _col(16)
LN1B = _col(16)
LN2G = _col(16)
LN2B = _col(16)
B2 = _col(16)
B1 = _col(64)
NV_IN = _c
HBA = _col(2 * H)
HBX = _col(2 * H)
CC = _col(2 * H)
HC = _col(2 * H)
TMP0 = _col(2 * H)
TMP1 = _col(2 * H)
TMP2 = _col(2 * H)
TMP3 = _col(2 * H)
TMP4 = _col(2 * H)
NV = _c

SB_BASE = 16512
SB_END = 229376


class Buf:
    def __init__(self, t=None):
        self.t = t
        self.w = None
        self.r = {}
        self.sem = None


class Trk:
    def __init__(self, nc):
        self.nc = nc
        self.e = {}
        for name, eng in (("pe", nc.tensor), ("act", nc.scalar), ("dve", nc.vector),
                          ("pool", nc.gpsimd), ("sp", nc.sync)):
            self.e[name] = dict(eng=eng, sem=nc.alloc_semaphore("s_" + name), cnt=0, seen={})
        self.dsem = {}
        self.nsem = 0

    def new_dma_sem(self):
        self.nsem += 1
        s = self.nc.alloc_semaphore("d%d" % self.nsem)
        self.dsem[id(s)] = [s, 0]
        return s

    def wait(self, en, toks):
        e = self.e[en]
        need = {}
        for t in toks:
            if t is None:
                continue
            sem, val = t
            k = id(sem)
            if k not in need or need[k][1] < val:
                need[k] = (sem, val)
        for k, (sem, val) in need.items():
            if e["seen"].get(k, 0) < val:
                e["eng"].wait_ge(sem, val)
                e["seen"][k] = val

    @staticmethod
    def _deps(reads, writes, extra):
        deps = list(extra)
        for b in reads:
            deps.append(b.w)
        for b in writes:
            deps.append(b.w)
            deps.extend(b.r.values())
        return deps

    @staticmethod
    def _note(tok, reads, writes):
        for b in reads:
            k = id(tok[0])
            if k not in b.r or b.r[k][1] < tok[1]:
                b.r[k] = tok
        for b in writes:
            b.w = tok
            b.r = {}

    def op(self, en, fn, reads=(), writes=(), extra=()):
        self.wait(en, self._deps(reads, writes, extra))
        e = self.e[en]
        ins = fn()
        e["cnt"] += 1
        ins.then_inc(e["sem"], 1)
        tok = (e["sem"], e["cnt"])
        self._note(tok, reads, writes)
        return tok

    def mm(self, out_ap, pairs, reads=(), writes=(), extra=()):
        self.wait("pe", self._deps(reads, writes, extra))
        e = self.e["pe"]
        n = len(pairs)
        ins = None
        for i, (l, r) in enumerate(pairs):
            ins = self.nc.tensor.matmul(out_ap, lhsT=l, rhs=r, start=(i == 0), stop=(i == n - 1))
        e["cnt"] += 1
        ins.then_inc(e["sem"], 1)
        tok = (e["sem"], e["cnt"])
        self._note(tok, reads, writes)
        return tok

    def dma(self, qn, pairs, sem, reads=(), writes=(), extra=()):
        self.wait(qn, self._deps(reads, writes, extra))
        q = self.e[qn]["eng"]
        rec = self.dsem[id(sem)]
        for (o, i) in pairs:
            q.dma_start(out=o, in_=i).then_inc(sem, 16)
            rec[1] += 16
        tok = (sem, rec[1])
        self._note(tok, reads, writes)
        return tok

    def barrier(self):
        toks = [(e["sem"], e["cnt"]) for e in self.e.values() if e["cnt"] > 0]
        toks += [(s, c) for s, c in self.dsem.values() if c > 0]
        for en in self.e:
            self.wait(en, toks)


class SbAlloc:
    def __init__(self, nc, base, end):
        self.nc, self.base, self.end, self.cur, self.n = nc, base, end, base, 0

    def mark(self):
        return self.cur

    def reset(self, m):
        self.cur = m

    def alloc(self, shape, dt):
        esz = 2 if dt == BF16 else 4
        per = 1
        for s in shape[1:]:
            per *= s
        nbytes = (per * esz + 63) // 64 * 64
        assert self.cur + nbytes <= self.end, ("SBUF overflow", self.cur, nbytes, self.end)
        self.n += 1
        t = self.nc.alloc_sbuf_tensor_at("sb%d" % self.n, list(shape), dt, offset=self.cur)
        self.cur += nbytes
        return t


def build_nc(debug=False, stop=None, nheads=H, nconv=16, ntiles=NTILES):
    nc = bass.Bass("TRN2", target_bir_lowering=False)
    xT_d = nc.dram_tensor("xT", [D, S], F32, kind="ExternalInput").ap()
    win_d = nc.dram_tensor("w_in_s", [100, 128, 2048], F32, kind="ExternalInput").ap()
    wg_d = nc.dram_tensor("w_g_s", [H, 128, 512], F32, kind="ExternalInput").ap()
    wlo_d = nc.dram_tensor("w_lo_s", [16, 2, 128, 1280], F32, kind="ExternalInput").ap()
    wco_d = nc.dram_tensor("w_co_s", [16, 128, 2048], F32, kind="ExternalInput").ap()
    wo_d = nc.dram_tensor("w_o_s", [16, 128, 2048], F32, kind="ExternalInput").ap()
    w1_d = nc.dram_tensor("w1_s", [64, 128, 2048], F32, kind="ExternalInput").ap()
    w2_d = nc.dram_tensor("w2_s", [NFG, 16, 128, FG * 128], F32, kind="ExternalInput").ap()
    vec_d = nc.dram_tensor("vecs", [128, NV_IN], F32, kind="ExternalInput").ap()
    outT_d = nc.dram_tensor("outT", [D, S], F32, kind="ExternalOutput").ap()
    hs_d = nc.dram_tensor("hs_scr", [H, 128, S], BF16, kind="Internal").ap()
    bv_d = nc.dram_tensor("bv_scr", [16, 128, S], BF16, kind="Internal").ap()
    g_d = nc.dram_tensor("g_scr", [32, 128, S], F32, kind="Internal").ap()
    if debug:
        dbg_hs = nc.dram_tensor("dbg_hs", [H, 128, S], BF16, kind="ExternalOutput").ap()
        dbg_bv = nc.dram_tensor("dbg_bv", [16, 128, S], BF16, kind="ExternalOutput").ap()

    T = Trk(nc)
    A = SbAlloc(nc, SB_BASE, SB_END)
    act, dve, pool = nc.scalar, nc.vector, nc.gpsimd

    vec = A.alloc([128, NV], F32)
    vecB = Buf(vec)
    ones32 = A.alloc([128, 128], F32)
    onesB = Buf(ones32)
    NRING = 5
    wring = [Buf(A.alloc([128, 2560], BF16)) for _ in range(NRING)]
    for b in wring:
        b.sem = T.new_dma_sem()
    gring = [Buf(A.alloc([128, 512], BF16)) for _ in range(2)]
    for b in gring:
        b.sem = T.new_dma_sem()
    ring_i = [0]

    pre = {}

    def wload(pairs_fn, key=None):
        if key is not None and key in pre:
            b = pre.pop(key)
            b.live = False
            return b
        b = wring[ring_i[0] % len(wring)]
        ring_i[0] += 1
        assert not getattr(b, "live", False), "weight ring slot reloaded while its slab is still needed"
        T.dma("pool", pairs_fn(b.t), b.sem, writes=[b])
        return b

    def wprefetch(key, pairs_fn):
        pre[key] = wload(pairs_fn)
        pre[key].live = True

    psall = nc.alloc_psum_tensor("psall", [128, 8 * 512], F32)

    class BankView:
        def __init__(self, i):
            self.i = i

        def __getitem__(self, key):
            return psall[:, self.i * 512:(self.i + 1) * 512]

    banks = [Buf(BankView(i)) for i in range(8)]

    def bank2(i):
        return psall[:, i * 512:(i + 2) * 512]
    bank_i = [0]

    def next_bank():
        b = banks[bank_i[0] % 6]
        bank_i[0] += 1
        return b

    def V(c, n=1):
        return vec[:, c:c + n]

    csem = T.new_dma_sem()
    T.dma("sp", [(vec[:, 0:NV_IN], vec_d)], csem, writes=[vecB])
    T.op("dve", lambda: dve.memset(ones32[:], 1.0), writes=[onesB])
    T.op("dve", lambda: dve.tensor_scalar_mul(out=V(HBA, 2 * H), in0=V(BA, 2 * H), scalar1=0.5), reads=[vecB], writes=[vecB])
    T.op("dve", lambda: dve.tensor_scalar_mul(out=V(HBX, 2 * H), in0=V(BX, 2 * H), scalar1=0.5), reads=[vecB], writes=[vecB])
    n2 = 2 * H
    t0, t1, t2, t3, t4 = V(TMP0, n2), V(TMP1, n2), V(TMP2, n2), V(TMP3, n2), V(TMP4, n2)
    def vop(en, fn):
        return T.op(en, fn, reads=[vecB], writes=[vecB])
    vop("act", lambda: act.activation(out=t0, in_=V(LAM, n2), func=AF.Abs))
    vop("act", lambda: act.activation(out=t1, in_=t0, func=AF.Exp, scale=-1.0))
    vop("act", lambda: act.activation(out=t2, in_=t1, func=AF.Ln, bias=1.0, scale=1.0))
    vop("dve", lambda: dve.tensor_scalar(out=t3, in0=t1, scalar1=-1.0 / 6, scalar2=1.0 / 5, op0=ALU.mult, op1=ALU.add))
    for cst in (1.0 / 4, 1.0 / 3, 1.0 / 2, 1.0):
        vop("dve", lambda: dve.tensor_tensor(out=t3, in0=t3, in1=t1, op=ALU.mult))
        vop("dve", lambda cst=cst: dve.tensor_scalar(out=t3, in0=t3, scalar1=-1.0, scalar2=cst, op0=ALU.mult, op1=ALU.add))
    vop("dve", lambda: dve.tensor_tensor(out=t3, in0=t3, in1=t1, op=ALU.mult))
    vop("dve", lambda: dve.tensor_single_scalar(out=t4, in_=t1, scalar=0.1, op=ALU.is_lt))
    vop("dve", lambda: dve.tensor_tensor(out=t3, in0=t3, in1=t2, op=ALU.subtract))
    vop("dve", lambda: dve.tensor_tensor(out=t3, in0=t3, in1=t4, op=ALU.mult))
    vop("dve", lambda: dve.tensor_tensor(out=t3, in0=t3, in1=t2, op=ALU.add))
    vop("dve", lambda: dve.tensor_scalar(out=t0, in0=V(LAM, n2), scalar1=-1.0, scalar2=0.0, op0=ALU.mult, op1=ALU.max))
    vop("dve", lambda: dve.tensor_tensor(out=t3, in0=t3, in1=t0, op=ALU.add))
    vop("dve", lambda: dve.tensor_scalar_mul(out=V(CC, n2), in0=t3, scalar1=-8.0))
    vop("dve", lambda: dve.tensor_scalar_mul(out=V(HC, n2), in0=t3, scalar1=-4.0))

    gmark = A.mark()
    if stop == 'const':
        T.barrier()
        return nc

    xT = A.alloc([128, KC, S], BF16)
    xTB = [Buf(xT) for _ in range(KC)]
    xsem = T.new_dma_sem()
    for kc in range(KC):
        xTB[kc].sem = xsem
    for kc in range(0, KC, 4):
        T.dma("pool", [(xT[:, k, :], xT_d[k * 128:(k + 1) * 128, :]) for k in range(kc, kc + 4)], xsem,
              writes=[xTB[k] for k in range(kc, kc + 4)])
    p1mark = A.mark()

    lxp = Buf(A.alloc([128, S + 4], F32))
    ub = [Buf(A.alloc([128, S], F32)) for _ in range(2)]
    ubf = Buf(A.alloc([128, S], BF16))
    thr = [[Buf(A.alloc([128, S], F32)) for _ in range(2)] for _ in range(2)]
    thi = [[Buf(A.alloc([128, S], F32)) for _ in range(2)] for _ in range(2)]
    aa = [Buf(A.alloc([128, S], F32)) for _ in range(2)]
    ubf.sem = T.new_dma_sem()
    hsb = ubf
    gst = [Buf(A.alloc([128, 512], F32)) for _ in range(2)]
    for b in gst:
        b.sem = T.new_dma_sem()
    hsD = [Buf() for _ in range(H)]
    T.op("dve", lambda: dve.memset(lxp.t[:, 0:2], 0.0), writes=[lxp])
    T.op("dve", lambda: dve.memset(lxp.t[:, S + 2:S + 4], 0.0), writes=[lxp])

    def rev(t, n):
        ap = t[:, 0:n]
        return bass.AP(t, ap.offset + (n - 1), [[ap.ap[0][0], 128], [-1, n]])

    wslab = {}
    def p1_prefetch_w(h):
        if h < nheads:
            wslab[h] = wload(lambda t, h=h: [(t[:, 0:2048], win_d[h, :, :])])
            wslab[h].live = True

    def p1_prefetch_g(h):
        if h < nheads:
            gb = gring[h % 2]
            T.dma("pool", [(gb.t[:, :], wg_d[h, :, :])], gb.sem, writes=[gb])

    def p1_inproj(h, chunks=(0, 1, 2, 3)):
        if h >= nheads:
            return
        wb = wslab[h]
        if 3 in chunks:
            wb.live = False
        for n in chunks:
            bk = banks[n]
            T.mm(bk.t[:, :], [(wb.t[:, k * 128:(k + 1) * 128], xT[:, k, n * 512:(n + 1) * 512]) for k in range(KC)],
                 reads=[wb] + xTB, writes=[bk])

    def p1_evac(h):
        U = ub[h % 2]
        cw2, cb = V(LCW + h * 4 + 2), V(LCB + h)
        for pr in (0, 2):
            bks = [banks[pr], banks[pr + 1]]
            T.op("act", lambda pr=pr: act.activation(out=lxp.t[:, 2 + pr * 512:2 + (pr + 2) * 512], in_=bank2(pr),
                                                     func=AF.Identity), reads=bks, writes=[lxp])
            T.op("act", lambda pr=pr: act.activation(out=U.t[:, pr * 512:(pr + 2) * 512], in_=bank2(pr),
                                                     func=AF.Identity, bias=cb, scale=cw2), reads=bks + [vecB], writes=[U])

    def p1_conv(h):
        U = ub[h % 2]
        cw = lambda j, h=h: V(LCW + h * 4 + j)
        for j in (0, 1, 3):
            T.op("dve", lambda j=j: dve.scalar_tensor_tensor(out=U.t[:, :], in0=lxp.t[:, j:j + S], scalar=cw(j), in1=U.t[:, :],
                                                             op0=ALU.mult, op1=ALU.add), reads=[lxp, vecB, U], writes=[U])

    def p1_cast(h):
        U = ub[h % 2]
        T.op("act", lambda: act.activation(out=ubf.t[:, :], in_=U.t[:, :], func=AF.Identity), reads=[U], writes=[ubf])

    def p1_gates(h):
        s = h % 2
        gb = gring[s]
        for d in range(2):
            for n in range(NCH):
                sl = slice(n * 512, (n + 1) * 512)
                for gi, (bk, dst, bcol) in enumerate(((banks[4], thr[s][d], HBA), (banks[5], thi[s][d], HBX))):
                    T.mm(bk.t[:, :], [(gb.t[:, (2 * d + gi) * 128:(2 * d + gi + 1) * 128], ubf.t[:, sl])],
                         reads=[gb, ubf], writes=[bk])
                    T.op("act", lambda bk=bk, dst=dst, bcol=bcol, sl=sl, d=d: act.activation(
                        out=dst.t[:, sl], in_=bk.t[:, :], func=AF.Tanh, bias=V(bcol + d * H + h), scale=0.5),
                        reads=[bk, vecB], writes=[dst])

    fill_units = [(kind, m, n) for m in range(16) for kind in range(2) for n in range(NCH)]
    fill_slabs = [(kind, m) for m in range(16) for kind in range(2)]
    fill_state = dict(pos=0, cnt=0, loaded=0)
    fill_buf = {}

    def fill_ensure(j):
        while fill_state["loaded"] <= j and fill_state["loaded"] < len(fill_slabs):
            kind, m = fill_slabs[fill_state["loaded"]]
            fill_buf[fill_state["loaded"]] = wload(lambda t, kind=kind, m=m: [(t[:, 0:2048], win_d[68 + 16 * kind + m, :, :])])
            fill_buf[fill_state["loaded"]].live = True
            fill_state["loaded"] += 1

    fill_pending = []

    def p1_fill_evac():
        while fill_pending:
            bk, st, kind, m, n = fill_pending.pop(0)
            T.op("dve", lambda bk=bk, st=st: dve.tensor_copy(out=st.t[:, :], in_=bk.t[:, :]), reads=[bk], writes=[st])
            T.dma("sp", [(g_d[16 * kind + m, :, n * 512:(n + 1) * 512], st.t[:, :])], st.sem, reads=[st])

    def p1_fill(k):
        p1_fill_evac()
        for _ in range(min(k, 2)):
            if fill_state["pos"] >= len(fill_units):
                return
            kind, m, n = fill_units[fill_state["pos"]]
            j = fill_state["pos"] // NCH
            fill_state["pos"] += 1
            fill_ensure(j + 1)
            wb = fill_buf[j]
            if n == NCH - 1:
                wb.live = False
            c = fill_state["cnt"]
            fill_state["cnt"] += 1
            bk = banks[6 + c % 2]
            st = gst[c % 2]
            T.mm(bk.t[:, :], [(wb.t[:, k2 * 128:(k2 + 1) * 128], xT[:, k2, n * 512:(n + 1) * 512]) for k2 in range(KC)],
                 reads=[wb] + xTB, writes=[bk])
            fill_pending.append((bk, st, kind, m, n))

    def p1_exp(h):
        TR = thr[h % 2]
        for d in range(2):
            hc = V(HC + d * H + h)
            T.op("act", lambda d=d, hc=hc: act.activation(out=aa[d].t[:, :], in_=TR[d].t[:, :], func=AF.Exp, bias=hc, scale=hc),
                 reads=[TR[d], vecB], writes=[aa[d]])

    def p1_asq(h):
        TR = thr[h % 2]
        for d in range(2):
            cc = V(CC + d * H + h)
            T.op("act", lambda d=d, cc=cc: act.activation(out=TR[d].t[:, :], in_=TR[d].t[:, :], func=AF.Exp, bias=cc, scale=cc),
                 reads=[vecB], writes=[TR[d]])

    def p1_sqrt(h):
        TR = thr[h % 2]
        for d in range(2):
            T.op("act", lambda d=d: act.activation(out=TR[d].t[:, :], in_=TR[d].t[:, :], func=AF.Sqrt, bias=0.25, scale=-0.25),
                 writes=[TR[d]])

    def p1_g1(h):
        U = ub[h % 2]
        TI = thi[h % 2]
        for d in range(2):
            T.op("dve", lambda d=d: dve.scalar_tensor_tensor(out=TI[d].t[:, :], in0=TI[d].t[:, :], scalar=1.0, in1=U.t[:, :],
                                                             op0=ALU.add, op1=ALU.mult), reads=[U], writes=[TI[d]])

    def p1_g2(h):
        TR, TI = thr[h % 2], thi[h % 2]
        for d in range(2):
            T.op("dve", lambda d=d: dve.tensor_tensor(out=TI[d].t[:, :], in0=TI[d].t[:, :], in1=TR[d].t[:, :], op=ALU.mult),
                 reads=[TR[d]], writes=[TI[d]])

    def p1_scan(h):
        TR, TI = thr[h % 2], thi[h % 2]
        T.op("dve", lambda: dve.tensor_tensor_scan(out=TR[0].t[:, :], data0=aa[0].t[:, :], data1=TI[0].t[:, :], initial=0.0,
                                                   op0=ALU.mult, op1=ALU.add), reads=[aa[0], TI[0]], writes=[TR[0]])
        T.op("dve", lambda: dve.tensor_tensor_scan(out=rev(TR[1].t, S), data0=rev(aa[1].t, S), data1=rev(TI[1].t, S),
                                                   initial=0.0, op0=ALU.mult, op1=ALU.add),
             reads=[aa[1], TI[1]], writes=[TR[1]])

    def p1_sum_store(h):
        TR = thr[h % 2]
        T.op("dve", lambda: dve.tensor_tensor(out=hsb.t[:, :], in0=TR[0].t[:, :], in1=TR[1].t[:, :], op=ALU.add),
             reads=[TR[0], TR[1]], writes=[hsb])
        T.dma("sp", [(hs_d[h, :, :], hsb.t[:, :])], hsb.sem, reads=[hsb], writes=[hsD[h]])

    if nheads > 0:
        for hh in range(3):
            p1_prefetch_w(hh)
        p1_prefetch_g(0)
        p1_prefetch_g(1)
        fill_ensure(0)
        p1_inproj(0)
        p1_evac(0)
        p1_inproj(1)
        p1_conv(0)
        p1_cast(0)
        p1_gates(0)
    for h in range(nheads):
        nx = h + 1 < nheads
        p1_prefetch_w(h + 3)
        p1_fill(2)
        p1_exp(h)
        p1_asq(h)
        if nx:
            p1_evac(h + 1)
            p1_inproj(h + 2, (0, 1))
            p1_conv(h + 1)
        p1_g1(h)
        p1_sqrt(h)
        p1_g2(h)
        if nx:
            p1_cast(h + 1)
            p1_prefetch_g(h + 2)
            p1_gates(h + 1)
            p1_inproj(h + 2, (2, 3))
        p1_fill(2)
        p1_scan(h)
        p1_sum_store(h)
        p1_fill(2)
    while fill_state["pos"] < len(fill_units):
        p1_fill(2)
    p1_fill_evac()

    xc = [ub[0], ub[0]]
    cxp = [lxp, lxp]
    T.op("dve", lambda: dve.memset(lxp.t[:, S + 1:S + 2], 0.0), writes=[lxp])
    vv = [ub[1], ub[1]]
    bvb = [ubf, ubf]
    bvD = [Buf() for _ in range(16)]

    if stop == 'p1':
        if debug:
            dsem = T.new_dma_sem()
            T.dma("sp", [(dbg_hs[0:nheads], hs_d[0:nheads])], dsem)
            T.barrier()
        return nc
    for c in range(nconv):
        s = c % 2
        XC, CX, VV, BV = xc[s], cxp[s], vv[s], bvb[s]
        wb = wload(lambda t, c=c: [(t[:, 0:2048], win_d[20 + c, :, :])])
        for n in range(NCH):
            bk = next_bank()
            T.mm(bk.t[:, :], [(wb.t[:, k * 128:(k + 1) * 128], xT[:, k, n * 512:(n + 1) * 512]) for k in range(KC)],
                 reads=[wb] + xTB, writes=[bk])
            T.op("act", lambda bk=bk, n=n: act.activation(out=XC.t[:, n * 512:(n + 1) * 512], in_=bk.t[:, :], func=AF.Identity),
                 reads=[bk], writes=[XC])
        wb = wload(lambda t, c=c: [(t[:, 0:2048], win_d[52 + c, :, :])])
        for n in range(NCH):
            bk = next_bank()
            T.mm(bk.t[:, :], [(wb.t[:, k * 128:(k + 1) * 128], xT[:, k, n * 512:(n + 1) * 512]) for k in range(KC)],
                 reads=[wb] + xTB, writes=[bk])
            T.op("dve", lambda bk=bk, n=n: dve.tensor_tensor(out=CX.t[:, 1 + n * 512:1 + (n + 1) * 512], in0=bk.t[:, :],
                                                             in1=XC.t[:, n * 512:(n + 1) * 512], op=ALU.mult),
                 reads=[bk, XC], writes=[CX])
        sw = lambda j, c=c: V(SCW + c * 3 + j)
        T.op("dve", lambda: dve.tensor_scalar(out=VV.t[:, :], in0=CX.t[:, 1:1 + S], scalar1=sw(1), scalar2=V(SCB + c),
                                              op0=ALU.mult, op1=ALU.add), reads=[CX, vecB], writes=[VV])
        for j in (0, 2):
            T.op("dve", lambda j=j: dve.scalar_tensor_tensor(out=VV.t[:, :], in0=CX.t[:, j:j + S], scalar=sw(j), in1=VV.t[:, :],
                                                             op0=ALU.mult, op1=ALU.add), reads=[CX, vecB, VV], writes=[VV])
        wb = wload(lambda t, c=c: [(t[:, 0:2048], win_d[36 + c, :, :])])
        for n in range(NCH):
            bk = next_bank()
            T.mm(bk.t[:, :], [(wb.t[:, k * 128:(k + 1) * 128], xT[:, k, n * 512:(n + 1) * 512]) for k in range(KC)],
                 reads=[wb] + xTB, writes=[bk])
            T.op("dve", lambda bk=bk, n=n: dve.tensor_tensor(out=BV.t[:, n * 512:(n + 1) * 512], in0=bk.t[:, :],
                                                             in1=VV.t[:, n * 512:(n + 1) * 512], op=ALU.mult),
                 reads=[bk, VV], writes=[BV])
        T.dma("sp", [(bv_d[c, :, :], BV.t[:, :])], BV.sem, reads=[BV], writes=[bvD[c]])

    if ntiles > 0 and stop is None:
        wprefetch(("yl", 0), lambda t: [(t[:, 0:1280], wlo_d[0, 0, :, :]), (t[:, 1280:2560], wlo_d[0, 1, :, :])])
        wprefetch(("yc", 0), lambda t: [(t[:, 0:2048], wco_d[0, :, :])])
    T.barrier()
    if debug and nheads > 0 and nconv > 0:
        dsem = T.new_dma_sem()
        T.dma("sp", [(dbg_hs[0:nheads], hs_d[0:nheads]), (dbg_bv[0:nconv], bv_d[0:nconv])], dsem)
        T.barrier()
    if stop == 'p2':
        return nc
    A.reset(gmark)
    extra = [Buf(A.alloc([128, 2560], BF16)) for _ in range(2)]
    for b in extra:
        b.sem = T.new_dma_sem()
    r = ring_i[0] % len(wring)
    wring[:] = extra + wring[r:] + wring[:r]
    ring_i[0] = 0
    hst = A.alloc([128, H, NT], BF16); hstB = Buf(hst); hstB.sem = T.new_dma_sem()
    bvt = A.alloc([128, 16, NT], BF16); bvtB = Buf(bvt); bvtB.sem = T.new_dma_sem()
    mg = A.alloc([128, 16, NT], BF16); mgB = [Buf(mg) for _ in range(16)]
    h1 = A.alloc([128, 16, NT], F32); h1B = [Buf(h1) for _ in range(16)]
    h1bf = A.alloc([128, 16, NT], BF16); h1bfB = [Buf(h1bf) for _ in range(16)]
    ff = [A.alloc([128, FG, NT], BF16) for _ in range(2)]
    ffB = [[Buf(ff[i]) for _ in range(FG)] for i in range(2)]
    def small(n=2):
        return [Buf(A.alloc([128, NT], F32)) for _ in range(n)]
    sgA, sgB_, t1A, t2A, sqA, xfA, rlA, ltA = small(), small(), small(), small(), small(), small(), small(), small()
    for b in xfA + sgA + sgB_:
        b.sem = T.new_dma_sem()
    meanB, rstdB, msqB, sdB = small(1)[0], small(1)[0], small(1)[0], small(1)[0]
    acc1, acc2 = small(1)[0], small(1)[0]
    osem = T.new_dma_sem()
    outB = Buf()
    cnt = dict(sga=0, sgb=0, ta=0, tb=0, sq=0, xf=0, rl=0, lt=0)

    def rot(lst, key):
        b = lst[cnt[key] % len(lst)]
        cnt[key] += 1
        return b

    def ln_stat_chunk(src, srcB, e):
        if e == 0:
            T.op("dve", lambda: dve.tensor_copy(out=acc1.t[:, :], in_=src[:, 0, :]), reads=[srcB[0]], writes=[acc1])
            T.op("act", lambda: act.activation(out=acc2.t[:, :], in_=src[:, 0, :], func=AF.Square), reads=[srcB[0]], writes=[acc2])
            return
        sq = rot(sqA, "sq")
        T.op("act", lambda e=e, sq=sq: act.activation(out=sq.t[:, :], in_=src[:, e, :], func=AF.Square),
             reads=[srcB[e]], writes=[sq])
        T.op("dve", lambda e=e: dve.tensor_tensor(out=acc1.t[:, :], in0=acc1.t[:, :], in1=src[:, e, :], op=ALU.add),
             reads=[srcB[e]], writes=[acc1])
        T.op("dve", lambda sq=sq: dve.tensor_tensor(out=acc2.t[:, :], in0=acc2.t[:, :], in1=sq.t[:, :], op=ALU.add),
             reads=[sq], writes=[acc2])

    def ln_finish():
        S1, S2 = banks[6], banks[7]
        T.mm(S1.t[:, :], [(ones32[:, :], acc1.t[:, :])], reads=[onesB, acc1], writes=[S1])
        T.mm(S2.t[:, :], [(ones32[:, :], acc2.t[:, :])], reads=[onesB, acc2], writes=[S2])
        T.op("dve", lambda: dve.tensor_scalar_mul(out=meanB.t[:, :], in0=S1.t[:, :], scalar1=1.0 / D), reads=[S1], writes=[meanB])
        T.op("dve", lambda: dve.tensor_tensor(out=msqB.t[:, :], in0=meanB.t[:, :], in1=meanB.t[:, :], op=ALU.mult),
             reads=[meanB], writes=[msqB])
        T.op("dve", lambda: dve.scalar_tensor_tensor(out=msqB.t[:, :], in0=S2.t[:, :], scalar=1.0 / D, in1=msqB.t[:, :],
                                                     op0=ALU.mult, op1=ALU.subtract), reads=[S2], writes=[msqB])
        T.op("act", lambda: act.activation(out=sdB.t[:, :], in_=msqB.t[:, :], func=AF.Sqrt, bias=V(TMP0, 1), scale=1.0),
             reads=[msqB, vecB], writes=[sdB])
        T.op("dve", lambda: dve.reciprocal(out=rstdB.t[:, :], in_=sdB.t[:, :]), reads=[sdB], writes=[rstdB])

    LAG = 2

    def ln_apply_pre(src, srcB, e, gcol):
        lt = rot(ltA, "lt")
        T.op("dve", lambda: dve.tensor_tensor(out=lt.t[:, :], in0=src[:, e, :], in1=meanB.t[:, :], op=ALU.subtract),
             reads=[srcB[e], meanB], writes=[lt])
        T.op("dve", lambda: dve.scalar_tensor_tensor(out=lt.t[:, :], in0=lt.t[:, :], scalar=V(gcol + e), in1=rstdB.t[:, :],
                                                     op0=ALU.mult, op1=ALU.mult), reads=[rstdB, vecB], writes=[lt])
        return lt

    T.op("dve", lambda: dve.memset(V(TMP0, 1), LN_EPS), writes=[vecB])

    def tsl_of(tau):
        return slice(tau * NT, (tau + 1) * NT)

    def tile_loads(tau):
        tsl = tsl_of(tau)
        T.dma("sp", [(hst[:, h0:h0 + 10, :], hs_d[h0:h0 + 10, :, tsl].rearrange("h p t -> p h t")) for h0 in (0, 10)],
              hstB.sem, reads=hsD, writes=[hstB])
        T.dma("sp", [(bvt[:, c0:c0 + 8, :], bv_d[c0:c0 + 8, :, tsl].rearrange("c p t -> p c t")) for c0 in (0, 8)],
              bvtB.sem, reads=bvD, writes=[bvtB])

    def p3(ms, tau):
        tsl = tsl_of(tau)
        for m in ms:
            sgl, sgc, ta, tb = rot(sgA, "sga"), rot(sgB_, "sgb"), rot(t1A, "ta"), rot(t2A, "tb")
            T.dma("sp", [(sgl.t[:, :], g_d[m, :, tsl])], sgl.sem, writes=[sgl])
            T.dma("sp", [(sgc.t[:, :], g_d[16 + m, :, tsl])], sgc.sem, writes=[sgc])
            T.op("act", lambda: act.activation(out=sgl.t[:, :], in_=sgl.t[:, :], func=AF.Sigmoid), writes=[sgl])
            T.op("act", lambda: act.activation(out=sgc.t[:, :], in_=sgc.t[:, :], func=AF.Sigmoid), writes=[sgc])
            wyl = wload(lambda t, m=m: [(t[:, 0:1280], wlo_d[m, 0, :, :]), (t[:, 1280:2560], wlo_d[m, 1, :, :])], key=("yl", m))
            bk = next_bank()
            T.mm(bk.t[:, :], [(wyl.t[:, k * 128:(k + 1) * 128], hst[:, k, :]) for k in range(H)], reads=[wyl, hstB], writes=[bk])
            T.op("dve", lambda bk=bk: dve.tensor_tensor(out=ta.t[:, :], in0=bk.t[:, :], in1=sgl.t[:, :], op=ALU.mult),
                 reads=[bk, sgl], writes=[ta])
            wyc = wload(lambda t, m=m: [(t[:, 0:2048], wco_d[m, :, :])], key=("yc", m))
            bk = next_bank()
            T.mm(bk.t[:, :], [(wyc.t[:, k * 128:(k + 1) * 128], bvt[:, k, :]) for k in range(16)], reads=[wyc, bvtB], writes=[bk])
            T.op("dve", lambda bk=bk: dve.tensor_tensor(out=tb.t[:, :], in0=bk.t[:, :], in1=sgc.t[:, :], op=ALU.mult),
                 reads=[bk, sgc], writes=[tb])
            T.op("dve", lambda m=m: dve.tensor_tensor(out=mg[:, m, :], in0=ta.t[:, :], in1=tb.t[:, :], op=ALU.add),
                 reads=[ta, tb], writes=[mgB[m]])

    def p4(tau):
        tsl = tsl_of(tau)
        for e in range(16):
            wb = wload(lambda t, e=e: [(t[:, 0:2048], wo_d[e, :, :])])
            xf = rot(xfA, "xf")
            T.dma("sp", [(xf.t[:, :], xT_d[e * 128:(e + 1) * 128, tsl])], xf.sem, writes=[xf])
            bk = next_bank()
            T.mm(bk.t[:, :], [(wb.t[:, k * 128:(k + 1) * 128], mg[:, k, :]) for k in range(16)], reads=[wb] + mgB, writes=[bk])
            T.op("dve", lambda e=e, bk=bk, xf=xf: dve.scalar_tensor_tensor(out=h1[:, e, :], in0=xf.t[:, :], scalar=ALPHA,
                                                                          in1=bk.t[:, :], op0=ALU.mult, op1=ALU.add),
                 reads=[xf, bk], writes=[h1B[e]], extra=([(osem, 16 * 16 * tau)] if tau > 0 else []))
            if e >= LAG:
                ln_stat_chunk(h1, h1B, e - LAG)
        for e in range(16 - LAG, 16):
            ln_stat_chunk(h1, h1B, e)
        ln_finish()

    def ln1_apply(tau, nxt):
        for e in range(16):
            lt = ln_apply_pre(h1, h1B, e, LN1G)
            T.op("act", lambda e=e, lt=lt: act.activation(out=h1[:, e, :], in_=lt.t[:, :], func=AF.Identity, bias=V(LN1B + e)),
                 reads=[lt, vecB], writes=[h1B[e]])
            T.op("act", lambda e=e, lt=lt: act.activation(out=h1bf[:, e, :], in_=lt.t[:, :], func=AF.Identity, bias=V(LN1B + e)),
                 reads=[lt, vecB], writes=[h1bfB[e]])
            if nxt and e % 2 == 1:
                p3([e // 2], tau + 1)

    def ff_group(g):
        fb = g % 2
        for j in range(FG):
            f = g * FG + j
            wb = wload(lambda t, f=f: [(t[:, 0:2048], w1_d[f, :, :])])
            bk = next_bank()
            T.mm(bk.t[:, :], [(wb.t[:, k * 128:(k + 1) * 128], h1bf[:, k, :]) for k in range(16)], reads=[wb] + h1bfB, writes=[bk])
            rl = rot(rlA, "rl")
            T.op("act", lambda bk=bk, rl=rl, f=f: act.activation(out=rl.t[:, :], in_=bk.t[:, :], func=AF.Relu, bias=V(B1 + f)),
                 reads=[bk, vecB], writes=[rl])
            T.op("dve", lambda rl=rl, fb=fb, j=j: dve.tensor_tensor(out=ff[fb][:, j, :], in0=rl.t[:, :], in1=rl.t[:, :], op=ALU.mult),
                 reads=[rl], writes=[ffB[fb][j]])

    def out_group(g):
        fb = g % 2
        for e2 in range(8):
            wb = wload(lambda t, e2=e2, g=g: [(t[:, 0:2 * FG * 128].rearrange("p (m j) -> p m j", m=2),
                                               w2_d[g, 2 * e2:2 * e2 + 2, :, :].rearrange("m p j -> p m j"))])
            for q in range(2):
                e = 2 * e2 + q
                bk = next_bank()
                T.mm(bk.t[:, :], [(wb.t[:, q * FG * 128 + k * 128:q * FG * 128 + (k + 1) * 128], ff[fb][:, k, :]) for k in range(FG)],
                     reads=[wb] + ffB[fb], writes=[bk])
                if g == 0:
                    T.op("dve", lambda e=e, bk=bk: dve.scalar_tensor_tensor(out=h1[:, e, :], in0=h1[:, e, :], scalar=ALPHA, in1=bk.t[:, :],
                                                                          op0=ALU.mult, op1=ALU.add), reads=[bk], writes=[h1B[e]])
                elif g < NFG - 1:
                    T.op("dve", lambda e=e, bk=bk: dve.tensor_tensor(out=h1[:, e, :], in0=h1[:, e, :], in1=bk.t[:, :], op=ALU.add),
                         reads=[bk], writes=[h1B[e]])
                else:
                    T.op("dve", lambda e=e, bk=bk: dve.scalar_tensor_tensor(out=h1[:, e, :], in0=bk.t[:, :], scalar=V(B2 + e), in1=h1[:, e, :],
                                                                          op0=ALU.add, op1=ALU.add), reads=[bk, vecB], writes=[h1B[e]])
                    if e >= LAG:
                        ln_stat_chunk(h1, h1B, e - LAG)
        if g == NFG - 1:
            for e in range(16 - LAG, 16):
                ln_stat_chunk(h1, h1B, e)
            ln_finish()

    def p5():
        ff_group(0)
        for g in range(NFG):
            if g + 1 < NFG:
                ff_group(g + 1)
            out_group(g)

    def ln2_apply_store(tau, nxt):
        tsl = tsl_of(tau)
        for e in range(16):
            lt = ln_apply_pre(h1, h1B, e, LN2G)
            T.op("act", lambda e=e, lt=lt: act.activation(out=h1[:, e, :], in_=lt.t[:, :], func=AF.Identity, bias=V(LN2B + e)),
                 reads=[lt, vecB], writes=[h1B[e]])
            T.dma("sp", [(outT_d[e * 128:(e + 1) * 128, tsl], h1[:, e, :])], osem, reads=[h1B[e]])
            if nxt and e % 2 == 1:
                p3([8 + e // 2], tau + 1)

    if ntiles > 0:
        tile_loads(0)
        p3(range(16), 0)
    for tau in range(ntiles):
        nxt = tau + 1 < ntiles
        p4(tau)
        if nxt:
            tile_loads(tau + 1)
        ln1_apply(tau, nxt)
        p5()
        ln2_apply_store(tau, nxt)

    T.barrier()
    return nc


def _slabify(Wm, G):
    K, M = Wm.shape
    a = Wm.reshape(K // (128 * G), G, 128, M // 128, 128)
    a = a.transpose(3, 0, 2, 1, 4)
    return np.ascontiguousarray(a).reshape(M // 128, K // (128 * G), 128, G * 128)


def _prep_weights(inp):
    f = lambda k: np.asarray(inp[k], dtype=np.float32)
    w_in_s = _slabify(f("w_in"), 16).reshape(100, 128, 2048)
    w_lo_s = _slabify(f("w_lru_out"), 10)
    w_co_s = _slabify(f("w_conv_out"), 16).reshape(16, 128, 2048)
    w_o_s = _slabify(f("w_o"), 16).reshape(16, 128, 2048)
    w1_s = _slabify(f("mlp_w1"), 16).reshape(64, 128, 2048)
    w2_s = np.ascontiguousarray(_slabify(f("mlp_w2"), FG).transpose(1, 0, 2, 3))
    wa, wx = f("lru_w_a"), f("lru_w_x")
    g = np.stack([wa[0], wx[0], wa[1], wx[1]], axis=2)
    w_g_s = np.ascontiguousarray(g).reshape(H, 128, 512)
    vecs = np.zeros((128, NV_IN), np.float32)
    vecs[:, LCW:LCW + H * 4] = f("lru_conv_w").reshape(4, H, 128).transpose(2, 1, 0).reshape(128, H * 4)
    vecs[:, LCB:LCB + H] = f("lru_conv_b").reshape(H, 128).T
    vecs[:, BA:BA + 2 * H] = f("lru_b_a").reshape(2 * H, 128).T
    vecs[:, BX:BX + 2 * H] = f("lru_b_x").reshape(2 * H, 128).T
    vecs[:, LAM:LAM + 2 * H] = f("lru_lambda").reshape(2 * H, 128).T
    vecs[:, SCW:SCW + 48] = f("sc_conv_w").reshape(3, 16, 128).transpose(2, 1, 0).reshape(128, 48)
    vecs[:, SCB:SCB + 16] = f("sc_conv_b").reshape(16, 128).T
    vecs[:, LN1G:LN1G + 16] = f("ln1_g").reshape(16, 128).T
    vecs[:, LN1B:LN1B + 16] = f("ln1_b").reshape(16, 128).T
    vecs[:, LN2G:LN2G + 16] = f("ln2_g").reshape(16, 128).T
    vecs[:, LN2B:LN2B + 16] = f("ln2_b").reshape(16, 128).T
    vecs[:, B2:B2 + 16] = f("mlp_b2").reshape(16, 128).T
    vecs[:, B1:B1 + 64] = f("mlp_b1").reshape(64, 128).T
    return {"w_in_s": w_in_s, "w_g_s": w_g_s, "w_lo_s": w_lo_s, "w_co_s": w_co_s, "w_o_s": w_o_s,
            "w1_s": w1_s, "w2_s": w2_s, "vecs": vecs}


def kernel(**inputs):
    x = np.asarray(inputs["x"], dtype=np.float32)
    B = x.shape[0]
    wts = _prep_weights(inputs)
    nc = build_nc()
    in_maps = []
    for b in range(B):
        m = dict(wts)
        m["xT"] = np.ascontiguousarray(x[b].T)
        in_maps.append(m)
    res = run_bass_kernel_spmd(nc, in_maps, core_ids=list(range(B)))
    out = np.stack([np.ascontiguousarray(np.asarray(r["outT"]).T) for r in res.results], axis=0)
    return out.astype(np.float32, copy=False)
```

```python
import numpy as np
import concourse.bass as bass
import concourse.mybir as mybir
from concourse.bass_utils import run_bass_kernel_spmd

F32 = mybir.dt.float32
BF16 = mybir.dt.bfloat16
AF = mybir.ActivationFunctionType
ALU = mybir.AluOpType

D = 2048
S = 2048
LW = 2560
H = 20
CW = 2048
DFF = 8192
KC = D // 128
NT = 512
NTILES = S // NT
NCH = S // 512
ALPHA = float(2.0 ** 0.25)
LN_EPS = 1e-5
FG = 8
NFG = (DFF // 128) // FG

_c = 0
def _col(n):
    global _c
    r = _c
    _c += n
    return r
LCW = _col(H * 4)
LCB = _col(H)
BA = _col(2 * H)
BX = _col(2 * H)
LAM = _col(2 * H)
SCW = _col(16 * 3)
SCB = _col(16)
LN1G = _col(16)
LN1B = _col(16)
LN2G = _col(16)
LN2B = _col(16)
B2 = _col(16)
B1 = _col(64)
NV_IN = _c
HBA = _col(2 * H)
HBX = _col(2 * H)
CC = _col(2 * H)
HC = _col(2 * H)
TMP0 = _col(2 * H)
TMP1 = _col(2 * H)
TMP2 = _col(2 * H)
TMP3 = _col(2 * H)
TMP4 = _col(2 * H)
NV = _c

SB_BASE = 16512
SB_END = 229376


class Buf:
    def __init__(self, t=None):
        self.t = t
        self.w = None
        self.r = {}
        self.sem = None


class Trk:
    def __init__(self, nc):
        self.nc = nc
        self.e = {}
        for name, eng in (("pe", nc.tensor), ("act", nc.scalar), ("dve", nc.vector),
                          ("pool", nc.gpsimd), ("sp", nc.sync)):
            self.e[name] = dict(eng=eng, sem=nc.alloc_semaphore("s_" + name), cnt=0, seen={})
        self.dsem = {}
        self.nsem = 0

    def new_dma_sem(self):
        self.nsem += 1
        s = self.nc.alloc_semaphore("d%d" % self.nsem)
        self.dsem[id(s)] = [s, 0]
        return s

    def wait(self, en, toks):
        e = self.e[en]
        need = {}
        for t in toks:
            if t is None:
                continue
            sem, val = t
            k = id(sem)
            if k not in need or need[k][1] < val:
                need[k] = (sem, val)
        for k, (sem, val) in need.items():
            if e["seen"].get(k, 0) < val:
                e["eng"].wait_ge(sem, val)
                e["seen"][k] = val

    @staticmethod
    def _deps(reads, writes, extra):
        deps = list(extra)
        for b in reads:
            deps.append(b.w)
        for b in writes:
            deps.append(b.w)
            deps.extend(b.r.values())
        return deps

    @staticmethod
    def _note(tok, reads, writes):
        for b in reads:
            k = id(tok[0])
            if k not in b.r or b.r[k][1] < tok[1]:
                b.r[k] = tok
        for b in writes:
            b.w = tok
            b.r = {}

    def op(self, en, fn, reads=(), writes=(), extra=()):
        self.wait(en, self._deps(reads, writes, extra))
        e = self.e[en]
        ins = fn()
        e["cnt"] += 1
        ins.then_inc(e["sem"], 1)
        tok = (e["sem"], e["cnt"])
        self._note(tok, reads, writes)
        return tok

    def mm(self, out_ap, pairs, reads=(), writes=(), extra=()):
        self.wait("pe", self._deps(reads, writes, extra))
        e = self.e["pe"]
        n = len(pairs)
        ins = None
        for i, (l, r) in enumerate(pairs):
            ins = self.nc.tensor.matmul(out_ap, lhsT=l, rhs=r, start=(i == 0), stop=(i == n - 1))
        e["cnt"] += 1
        ins.then_inc(e["sem"], 1)
        tok = (e["sem"], e["cnt"])
        self._note(tok, reads, writes)
        return tok

    def dma(self, qn, pairs, sem, reads=(), writes=(), extra=()):
        self.wait(qn, self._deps(reads, writes, extra))
        q = self.e[qn]["eng"]
        rec = self.dsem[id(sem)]
        for (o, i) in pairs:
            q.dma_start(out=o, in_=i).then_inc(sem, 16)
            rec[1] += 16
        tok = (sem, rec[1])
        self._note(tok, reads, writes)
        return tok

    def barrier(self):
        toks = [(e["sem"], e["cnt"]) for e in self.e.values() if e["cnt"] > 0]
        toks += [(s, c) for s, c in self.dsem.values() if c > 0]
        for en in self.e:
            self.wait(en, toks)


class SbAlloc:
    def __init__(self, nc, base, end):
        self.nc, self.base, self.end, self.cur, self.n = nc, base, end, base, 0

    def mark(self):
        return self.cur

    def reset(self, m):
        self.cur = m

    def alloc(self, shape, dt):
        esz = 2 if dt == BF16 else 4
        per = 1
        for s in shape[1:]:
            per *= s
        nbytes = (per * esz + 63) // 64 * 64
        assert self.cur + nbytes <= self.end, ("SBUF overflow", self.cur, nbytes, self.end)
        self.n += 1
        t = self.nc.alloc_sbuf_tensor_at("sb%d" % self.n, list(shape), dt, offset=self.cur)
        self.cur += nbytes
        return t


def build_nc(debug=False, stop=None, nheads=H, nconv=16, ntiles=NTILES):
    nc = bass.Bass("TRN2", target_bir_lowering=False)
    xT_d = nc.dram_tensor("xT", [D, S], F32, kind="ExternalInput").ap()
    win_d = nc.dram_tensor("w_in_s", [100, 128, 2048], F32, kind="ExternalInput").ap()
    wg_d = nc.dram_tensor("w_g_s", [H, 128, 512], F32, kind="ExternalInput").ap()
    wlo_d = nc.dram_tensor("w_lo_s", [16, 2, 128, 1280], F32, kind="ExternalInput").ap()
    wco_d = nc.dram_tensor("w_co_s", [16, 128, 2048], F32, kind="ExternalInput").ap()
    wo_d = nc.dram_tensor("w_o_s", [16, 128, 2048], F32, kind="ExternalInput").ap()
    w1_d = nc.dram_tensor("w1_s", [64, 128, 2048], F32, kind="ExternalInput").ap()
    w2_d = nc.dram_tensor("w2_s", [NFG, 16, 128, FG * 128], F32, kind="ExternalInput").ap()
    vec_d = nc.dram_tensor("vecs", [128, NV_IN], F32, kind="ExternalInput").ap()
    outT_d = nc.dram_tensor("outT", [D, S], F32, kind="ExternalOutput").ap()
    hs_d = nc.dram_tensor("hs_scr", [H, 128, S], BF16, kind="Internal").ap()
    bv_d = nc.dram_tensor("bv_scr", [16, 128, S], BF16, kind="Internal").ap()
    g_d = nc.dram_tensor("g_scr", [32, 128, S], F32, kind="Internal").ap()
    if debug:
        dbg_hs = nc.dram_tensor("dbg_hs", [H, 128, S], BF16, kind="ExternalOutput").ap()
        dbg_bv = nc.dram_tensor("dbg_bv", [16, 128, S], BF16, kind="ExternalOutput").ap()

    T = Trk(nc)
    A = SbAlloc(nc, SB_BASE, SB_END)
    act, dve, pool = nc.scalar, nc.vector, nc.gpsimd

    vec = A.alloc([128, NV], F32)
    vecB = Buf(vec)
    ones32 = A.alloc([128, 128], F32)
    onesB = Buf(ones32)
    NRING = 5
    wring = [Buf(A.alloc([128, 2560], BF16)) for _ in range(NRING)]
    for b in wring:
        b.sem = T.new_dma_sem()
    gring = [Buf(A.alloc([128, 512], BF16)) for _ in range(2)]
    for b in gring:
        b.sem = T.new_dma_sem()
    ring_i = [0]

    pre = {}

    def wload(pairs_fn, key=None):
        if key is not None and key in pre:
            b = pre.pop(key)
            b.live = False
            return b
        b = wring[ring_i[0] % len(wring)]
        ring_i[0] += 1
        assert not getattr(b, "live", False), "weight ring slot reloaded while its slab is still needed"
        T.dma("pool", pairs_fn(b.t), b.sem, writes=[b])
        return b

    def wprefetch(key, pairs_fn):
        pre[key] = wload(pairs_fn)
        pre[key].live = True

    psall = nc.alloc_psum_tensor("psall", [128, 8 * 512], F32)

    class BankView:
        def __init__(self, i):
            self.i = i

        def __getitem__(self, key):
            return psall[:, self.i * 512:(self.i + 1) * 512]

    banks = [Buf(BankView(i)) for i in range(8)]

    def bank2(i):
        return psall[:, i * 512:(i + 2) * 512]
    bank_i = [0]

    def next_bank():
        b = banks[bank_i[0] % 6]
        bank_i[0] += 1
        return b

    def V(c, n=1):
        return vec[:, c:c + n]

    csem = T.new_dma_sem()
    T.dma("sp", [(vec[:, 0:NV_IN], vec_d)], csem, writes=[vecB])
    T.op("dve", lambda: dve.memset(ones32[:], 1.0), writes=[onesB])
    T.op("dve", lambda: dve.tensor_scalar_mul(out=V(HBA, 2 * H), in0=V(BA, 2 * H), scalar1=0.5), reads=[vecB], writes=[vecB])
    T.op("dve", lambda: dve.tensor_scalar_mul(out=V(HBX, 2 * H), in0=V(BX, 2 * H), scalar1=0.5), reads=[vecB], writes=[vecB])
    n2 = 2 * H
    t0, t1, t2, t3, t4 = V(TMP0, n2), V(TMP1, n2), V(TMP2, n2), V(TMP3, n2), V(TMP4, n2)
    def vop(en, fn):
        return T.op(en, fn, reads=[vecB], writes=[vecB])
    vop("act", lambda: act.activation(out=t0, in_=V(LAM, n2), func=AF.Abs))
    vop("act", lambda: act.activation(out=t1, in_=t0, func=AF.Exp, scale=-1.0))
    vop("act", lambda: act.activation(out=t2, in_=t1, func=AF.Ln, bias=1.0, scale=1.0))
    vop("dve", lambda: dve.tensor_scalar(out=t3, in0=t1, scalar1=-1.0 / 6, scalar2=1.0 / 5, op0=ALU.mult, op1=ALU.add))
    for cst in (1.0 / 4, 1.0 / 3, 1.0 / 2, 1.0):
        vop("dve", lambda: dve.tensor_tensor(out=t3, in0=t3, in1=t1, op=ALU.mult))
        vop("dve", lambda cst=cst: dve.tensor_scalar(out=t3, in0=t3, scalar1=-1.0, scalar2=cst, op0=ALU.mult, op1=ALU.add))
    vop("dve", lambda: dve.tensor_tensor(out=t3, in0=t3, in1=t1, op=ALU.mult))
    vop("dve", lambda: dve.tensor_single_scalar(out=t4, in_=t1, scalar=0.1, op=ALU.is_lt))
    vop("dve", lambda: dve.tensor_tensor(out=t3, in0=t3, in1=t2, op=ALU.subtract))
    vop("dve", lambda: dve.tensor_tensor(out=t3, in0=t3, in1=t4, op=ALU.mult))
    vop("dve", lambda: dve.tensor_tensor(out=t3, in0=t3, in1=t2, op=ALU.add))
    vop("dve", lambda: dve.tensor_scalar(out=t0, in0=V(LAM, n2), scalar1=-1.0, scalar2=0.0, op0=ALU.mult, op1=ALU.max))
    vop("dve", lambda: dve.tensor_tensor(out=t3, in0=t3, in1=t0, op=ALU.add))
    vop("dve", lambda: dve.tensor_scalar_mul(out=V(CC, n2), in0=t3, scalar1=-8.0))
    vop("dve", lambda: dve.tensor_scalar_mul(out=V(HC, n2), in0=t3, scalar1=-4.0))

    gmark = A.mark()
    if stop == 'const':
        T.barrier()
        return nc

    xT = A.alloc([128, KC, S], BF16)
    xTB = [Buf(xT) for _ in range(KC)]
    xsem = T.new_dma_sem()
    for kc in range(KC):
        xTB[kc].sem = xsem
    for kc in range(0, KC, 4):
        T.dma("pool", [(xT[:, k, :], xT_d[k * 128:(k + 1) * 128, :]) for k in range(kc, kc + 4)], xsem,
              writes=[xTB[k] for k in range(kc, kc + 4)])
    p1mark = A.mark()

    lxp = Buf(A.alloc([128, S + 4], F32))
    ub = [Buf(A.alloc([128, S], F32)) for _ in range(2)]
    ubf = Buf(A.alloc([128, S], BF16))
    thr = [[Buf(A.alloc([128, S], F32)) for _ in range(2)] for _ in range(2)]
    thi = [[Buf(A.alloc([128, S], F32)) for _ in range(2)] for _ in range(2)]
    aa = [Buf(A.alloc([128, S], F32)) for _ in range(2)]
    ubf.sem = T.new_dma_sem()
    hsb = ubf
    gst = [Buf(A.alloc([128, 512], F32)) for _ in range(2)]
    for b in gst:
        b.sem = T.new_dma_sem()
    hsD = [Buf() for _ in range(H)]
    T.op("dve", lambda: dve.memset(lxp.t[:, 0:2], 0.0), writes=[lxp])
    T.op("dve", lambda: dve.memset(lxp.t[:, S + 2:S + 4], 0.0), writes=[lxp])

    def rev(t, n):
        ap = t[:, 0:n]
        return bass.AP(t, ap.offset + (n - 1), [[ap.ap[0][0], 128], [-1, n]])

    wslab = {}
    def p1_prefetch_w(h):
        if h < nheads:
            wslab[h] = wload(lambda t, h=h: [(t[:, 0:2048], win_d[h, :, :])])
            wslab[h].live = True

    def p1_prefetch_g(h):
        if h < nheads:
            gb = gring[h % 2]
            T.dma("pool", [(gb.t[:, :], wg_d[h, :, :])], gb.sem, writes=[gb])

    def p1_inproj(h, chunks=(0, 1, 2, 3)):
        if h >= nheads:
            return
        wb = wslab[h]
        if 3 in chunks:
            wb.live = False
        for n in chunks:
            bk = banks[n]
            T.mm(bk.t[:, :], [(wb.t[:, k * 128:(k + 1) * 128], xT[:, k, n * 512:(n + 1) * 512]) for k in range(KC)],
                 reads=[wb] + xTB, writes=[bk])

    def p1_evac(h):
        U = ub[h % 2]
        cw2, cb = V(LCW + h * 4 + 2), V(LCB + h)
        for pr in (0, 2):
            bks = [banks[pr], banks[pr + 1]]
            T.op("act", lambda pr=pr: act.activation(out=lxp.t[:, 2 + pr * 512:2 + (pr + 2) * 512], in_=bank2(pr),
                                                     func=AF.Identity), reads=bks, writes=[lxp])
            T.op("act", lambda pr=pr: act.activation(out=U.t[:, pr * 512:(pr + 2) * 512], in_=bank2(pr),
                                                     func=AF.Identity, bias=cb, scale=cw2), reads=bks + [vecB], writes=[U])

    def p1_conv(h):
        U = ub[h % 2]
        cw = lambda j, h=h: V(LCW + h * 4 + j)
        for j in (0, 1, 3):
            T.op("dve", lambda j=j: dve.scalar_tensor_tensor(out=U.t[:, :], in0=lxp.t[:, j:j + S], scalar=cw(j), in1=U.t[:, :],
                                                             op0=ALU.mult, op1=ALU.add), reads=[lxp, vecB, U], writes=[U])

    def p1_cast(h):
        U = ub[h % 2]
        T.op("act", lambda: act.activation(out=ubf.t[:, :], in_=U.t[:, :], func=AF.Identity), reads=[U], writes=[ubf])

    def p1_gates(h):
        s = h % 2
        gb = gring[s]
        for d in range(2):
            for n in range(NCH):
                sl = slice(n * 512, (n + 1) * 512)
                for gi, (bk, dst, bcol) in enumerate(((banks[4], thr[s][d], HBA), (banks[5], thi[s][d], HBX))):
                    T.mm(bk.t[:, :], [(gb.t[:, (2 * d + gi) * 128:(2 * d + gi + 1) * 128], ubf.t[:, sl])],
                         reads=[gb, ubf], writes=[bk])
                    T.op("act", lambda bk=bk, dst=dst, bcol=bcol, sl=sl, d=d: act.activation(
                        out=dst.t[:, sl], in_=bk.t[:, :], func=AF.Tanh, bias=V(bcol + d * H + h), scale=0.5),
                        reads=[bk, vecB], writes=[dst])

    fill_units = [(kind, m, n) for m in range(16) for kind in range(2) for n in range(NCH)]
    fill_slabs = [(kind, m) for m in range(16) for kind in range(2)]
    fill_state = dict(pos=0, cnt=0, loaded=0)
    fill_buf = {}

    def fill_ensure(j):
        while fill_state["loaded"] <= j and fill_state["loaded"] < len(fill_slabs):
            kind, m = fill_slabs[fill_state["loaded"]]
            fill_buf[fill_state["loaded"]] = wload(lambda t, kind=kind, m=m: [(t[:, 0:2048], win_d[68 + 16 * kind + m, :, :])])
            fill_buf[fill_state["loaded"]].live = True
            fill_state["loaded"] += 1

    fill_pending = []

    def p1_fill_evac():
        while fill_pending:
            bk, st, kind, m, n = fill_pending.pop(0)
            T.op("dve", lambda bk=bk, st=st: dve.tensor_copy(out=st.t[:, :], in_=bk.t[:, :]), reads=[bk], writes=[st])
            T.dma("sp", [(g_d[16 * kind + m, :, n * 512:(n + 1) * 512], st.t[:, :])], st.sem, reads=[st])

    def p1_fill(k):
        p1_fill_evac()
        for _ in range(min(k, 2)):
            if fill_state["pos"] >= len(fill_units):
                return
            kind, m, n = fill_units[fill_state["pos"]]
            j = fill_state["pos"] // NCH
            fill_state["pos"] += 1
            fill_ensure(j + 1)
            wb = fill_buf[j]
            if n == NCH - 1:
                wb.live = False
            c = fill_state["cnt"]
            fill_state["cnt"] += 1
            bk = banks[6 + c % 2]
            st = gst[c % 2]
            T.mm(bk.t[:, :], [(wb.t[:, k2 * 128:(k2 + 1) * 128], xT[:, k2, n * 512:(n + 1) * 512]) for k2 in range(KC)],
                 reads=[wb] + xTB, writes=[bk])
            fill_pending.append((bk, st, kind, m, n))

    def p1_exp(h):
        TR = thr[h % 2]
        for d in range(2):
            hc = V(HC + d * H + h)
            T.op("act", lambda d=d, hc=hc: act.activation(out=aa[d].t[:, :], in_=TR[d].t[:, :], func=AF.Exp, bias=hc, scale=hc),
                 reads=[TR[d], vecB], writes=[aa[d]])

    def p1_asq(h):
        TR = thr[h % 2]
        for d in range(2):
            cc = V(CC + d * H + h)
            T.op("act", lambda d=d, cc=cc: act.activation(out=TR[d].t[:, :], in_=TR[d].t[:, :], func=AF.Exp, bias=cc, scale=cc),
                 reads=[vecB], writes=[TR[d]])

    def p1_sqrt(h):
        TR = thr[h % 2]
        for d in range(2):
            T.op("act", lambda d=d: act.activation(out=TR[d].t[:, :], in_=TR[d].t[:, :], func=AF.Sqrt, bias=0.25, scale=-0.25),
                 writes=[TR[d]])

    def p1_g1(h):
        U = ub[h % 2]
        TI = thi[h % 2]
        for d in range(2):
            T.op("dve", lambda d=d: dve.scalar_tensor_tensor(out=TI[d].t[:, :], in0=TI[d].t[:, :], scalar=1.0, in1=U.t[:, :],
                                                             op0=ALU.add, op1=ALU.mult), reads=[U], writes=[TI[d]])

    def p1_g2(h):
        TR, TI = thr[h % 2], thi[h % 2]
        for d in range(2):
            T.op("dve", lambda d=d: dve.tensor_tensor(out=TI[d].t[:, :], in0=TI[d].t[:, :], in1=TR[d].t[:, :], op=ALU.mult),
                 reads=[TR[d]], writes=[TI[d]])

    def p1_scan(h):
        TR, TI = thr[h % 2], thi[h % 2]
        T.op("dve", lambda: dve.tensor_tensor_scan(out=TR[0].t[:, :], data0=aa[0].t[:, :], data1=TI[0].t[:, :], initial=0.0,
                                                   op0=ALU.mult, op1=ALU.add), reads=[aa[0], TI[0]], writes=[TR[0]])
        T.op("dve", lambda: dve.tensor_tensor_scan(out=rev(TR[1].t, S), data0=rev(aa[1].t, S), data1=rev(TI[1].t, S),
                                                   initial=0.0, op0=ALU.mult, op1=ALU.add),
             reads=[aa[1], TI[1]], writes=[TR[1]])

    def p1_sum_store(h):
        TR = thr[h % 2]
        T.op("dve", lambda: dve.tensor_tensor(out=hsb.t[:, :], in0=TR[0].t[:, :], in1=TR[1].t[:, :], op=ALU.add),
             reads=[TR[0], TR[1]], writes=[hsb])
        T.dma("sp", [(hs_d[h, :, :], hsb.t[:, :])], hsb.sem, reads=[hsb], writes=[hsD[h]])

    if nheads > 0:
        for hh in range(3):
            p1_prefetch_w(hh)
        p1_prefetch_g(0)
        p1_prefetch_g(1)
        fill_ensure(0)
        p1_inproj(0)
        p1_evac(0)
        p1_inproj(1)
        p1_conv(0)
        p1_cast(0)
        p1_gates(0)
    for h in range(nheads):
        nx = h + 1 < nheads
        p1_prefetch_w(h + 3)
        p1_fill(2)
        p1_exp(h)
        p1_asq(h)
        if nx:
            p1_evac(h + 1)
            p1_inproj(h + 2, (0, 1))
            p1_conv(h + 1)
        p1_g1(h)
        p1_sqrt(h)
        p1_g2(h)
        if nx:
            p1_cast(h + 1)
            p1_prefetch_g(h + 2)
            p1_gates(h + 1)
            p1_inproj(h + 2, (2, 3))
        p1_fill(2)
        p1_scan(h)
        p1_sum_store(h)
        p1_fill(2)
    while fill_state["pos"] < len(fill_units):
        p1_fill(2)
    p1_fill_evac()

    xc = [ub[0], ub[0]]
    cxp = [lxp, lxp]
    T.op("dve", lambda: dve.memset(lxp.t[:, S + 1:S + 2], 0.0), writes=[lxp])
    vv = [ub[1], ub[1]]
    bvb = [ubf, ubf]
    bvD = [Buf() for _ in range(16)]

    if stop == 'p1':
        if debug:
            dsem = T.new_dma_sem()
            T.dma("sp", [(dbg_hs[0:nheads], hs_d[0:nheads])], dsem)
            T.barrier()
        return nc
    for c in range(nconv):
        s = c % 2
        XC, CX, VV, BV = xc[s], cxp[s], vv[s], bvb[s]
        wb = wload(lambda t, c=c: [(t[:, 0:2048], win_d[20 + c, :, :])])
        for n in range(NCH):
            bk = next_bank()
            T.mm(bk.t[:, :], [(wb.t[:, k * 128:(k + 1) * 128], xT[:, k, n * 512:(n + 1) * 512]) for k in range(KC)],
                 reads=[wb] + xTB, writes=[bk])
            T.op("act", lambda bk=bk, n=n: act.activation(out=XC.t[:, n * 512:(n + 1) * 512], in_=bk.t[:, :], func=AF.Identity),
                 reads=[bk], writes=[XC])
        wb = wload(lambda t, c=c: [(t[:, 0:2048], win_d[52 + c, :, :])])
        for n in range(NCH):
            bk = next_bank()
            T.mm(bk.t[:, :], [(wb.t[:, k * 128:(k + 1) * 128], xT[:, k, n * 512:(n + 1) * 512]) for k in range(KC)],
                 reads=[wb] + xTB, writes=[bk])
            T.op("dve", lambda bk=bk, n=n: dve.tensor_tensor(out=CX.t[:, 1 + n * 512:1 + (n + 1) * 512], in0=bk.t[:, :],
                                                             in1=XC.t[:, n * 512:(n + 1) * 512], op=ALU.mult),
                 reads=[bk, XC], writes=[CX])
        sw = lambda j, c=c: V(SCW + c * 3 + j)
        T.op("dve", lambda: dve.tensor_scalar(out=VV.t[:, :], in0=CX.t[:, 1:1 + S], scalar1=sw(1), scalar2=V(SCB + c),
                                              op0=ALU.mult, op1=ALU.add), reads=[CX, vecB], writes=[VV])
        for j in (0, 2):
            T.op("dve", lambda j=j: dve.scalar_tensor_tensor(out=VV.t[:, :], in0=CX.t[:, j:j + S], scalar=sw(j), in1=VV.t[:, :],
                                                             op0=ALU.mult, op1=ALU.add), reads=[CX, vecB, VV], writes=[VV])
        wb = wload(lambda t, c=c: [(t[:, 0:2048], win_d[36 + c, :, :])])
        for n in range(NCH):
            bk = next_bank()
            T.mm(bk.t[:, :], [(wb.t[:, k * 128:(k + 1) * 128], xT[:, k, n * 512:(n + 1) * 512]) for k in range(KC)],
                 reads=[wb] + xTB, writes=[bk])
            T.op("dve", lambda bk=bk, n=n: dve.tensor_tensor(out=BV.t[:, n * 512:(n + 1) * 512], in0=bk.t[:, :],
                                                             in1=VV.t[:, n * 512:(n + 1) * 512], op=ALU.mult),
                 reads=[bk, VV], writes=[BV])
        T.dma("sp", [(bv_d[c, :, :], BV.t[:, :])], BV.sem, reads=[BV], writes=[bvD[c]])

    if ntiles > 0 and stop is None:
        wprefetch(("yl", 0), lambda t: [(t[:, 0:1280], wlo_d[0, 0, :, :]), (t[:, 1280:2560], wlo_d[0, 1, :, :])])
        wprefetch(("yc", 0), lambda t: [(t[:, 0:2048], wco_d[0, :, :])])
    T.barrier()
    if debug and nheads > 0 and nconv > 0:
        dsem = T.new_dma_sem()
        T.dma("sp", [(dbg_hs[0:nheads], hs_d[0:nheads]), (dbg_bv[0:nconv], bv_d[0:nconv])], dsem)
        T.barrier()
    if stop == 'p2':
        return nc
    A.reset(gmark)
    extra = [Buf(A.alloc([128, 2560], BF16)) for _ in range(2)]
    for b in extra:
        b.sem = T.new_dma_sem()
    r = ring_i[0] % len(wring)
    wring[:] = extra + wring[r:] + wring[:r]
    ring_i[0] = 0
    hst = A.alloc([128, H, NT], BF16); hstB = Buf(hst); hstB.sem = T.new_dma_sem()
    bvt = A.alloc([128, 16, NT], BF16); bvtB = Buf(bvt); bvtB.sem = T.new_dma_sem()
    mg = A.alloc([128, 16, NT], BF16); mgB = [Buf(mg) for _ in range(16)]
    h1 = A.alloc([128, 16, NT], F32); h1B = [Buf(h1) for _ in range(16)]
    h1bf = A.alloc([128, 16, NT], BF16); h1bfB = [Buf(h1bf) for _ in range(16)]
    ff = [A.alloc([128, FG, NT], BF16) for _ in range(2)]
    ffB = [[Buf(ff[i]) for _ in range(FG)] for i in range(2)]
    def small(n=2):
        return [Buf(A.alloc([128, NT], F32)) for _ in range(n)]
    sgA, sgB_, t1A, t2A, sqA, xfA, rlA, ltA = small(), small(), small(), small(), small(), small(), small(), small()
    for b in xfA + sgA + sgB_:
        b.sem = T.new_dma_sem()
    meanB, rstdB, msqB, sdB = small(1)[0], small(1)[0], small(1)[0], small(1)[0]
    acc1, acc2 = small(1)[0], small(1)[0]
    osem = T.new_dma_sem()
    outB = Buf()
    cnt = dict(sga=0, sgb=0, ta=0, tb=0, sq=0, xf=0, rl=0, lt=0)

    def rot(lst, key):
        b = lst[cnt[key] % len(lst)]
        cnt[key] += 1
        return b

    def ln_stat_chunk(src, srcB, e):
        if e == 0:
            T.op("dve", lambda: dve.tensor_copy(out=acc1.t[:, :], in_=src[:, 0, :]), reads=[srcB[0]], writes=[acc1])
            T.op("act", lambda: act.activation(out=acc2.t[:, :], in_=src[:, 0, :], func=AF.Square), reads=[srcB[0]], writes=[acc2])
            return
        sq = rot(sqA, "sq")
        T.op("act", lambda e=e, sq=sq: act.activation(out=sq.t[:, :], in_=src[:, e, :], func=AF.Square),
             reads=[srcB[e]], writes=[sq])
        T.op("dve", lambda e=e: dve.tensor_tensor(out=acc1.t[:, :], in0=acc1.t[:, :], in1=src[:, e, :], op=ALU.add),
             reads=[srcB[e]], writes=[acc1])
        T.op("dve", lambda sq=sq: dve.tensor_tensor(out=acc2.t[:, :], in0=acc2.t[:, :], in1=sq.t[:, :], op=ALU.add),
             reads=[sq], writes=[acc2])

    def ln_finish():
        S1, S2 = banks[6], banks[7]
        T.mm(S1.t[:, :], [(ones32[:, :], acc1.t[:, :])], reads=[onesB, acc1], writes=[S1])
        T.mm(S2.t[:, :], [(ones32[:, :], acc2.t[:, :])], reads=[onesB, acc2], writes=[S2])
        T.op("dve", lambda: dve.tensor_scalar_mul(out=meanB.t[:, :], in0=S1.t[:, :], scalar1=1.0 / D), reads=[S1], writes=[meanB])
        T.op("dve", lambda: dve.tensor_tensor(out=msqB.t[:, :], in0=meanB.t[:, :], in1=meanB.t[:, :], op=ALU.mult),
             reads=[meanB], writes=[msqB])
        T.op("dve", lambda: dve.scalar_tensor_tensor(out=msqB.t[:, :], in0=S2.t[:, :], scalar=1.0 / D, in1=msqB.t[:, :],
                                                     op0=ALU.mult, op1=ALU.subtract), reads=[S2], writes=[msqB])
        T.op("act", lambda: act.activation(out=sdB.t[:, :], in_=msqB.t[:, :], func=AF.Sqrt, bias=V(TMP0, 1), scale=1.0),
             reads=[msqB, vecB], writes=[sdB])
        T.op("dve", lambda: dve.reciprocal(out=rstdB.t[:, :], in_=sdB.t[:, :]), reads=[sdB], writes=[rstdB])

    LAG = 2

    def ln_apply_pre(src, srcB, e, gcol):
        lt = rot(ltA, "lt")
        T.op("dve", lambda: dve.tensor_tensor(out=lt.t[:, :], in0=src[:, e, :], in1=meanB.t[:, :], op=ALU.subtract),
             reads=[srcB[e], meanB], writes=[lt])
        T.op("dve", lambda: dve.scalar_tensor_tensor(out=lt.t[:, :], in0=lt.t[:, :], scalar=V(gcol + e), in1=rstdB.t[:, :],
                                                     op0=ALU.mult, op1=ALU.mult), reads=[rstdB, vecB], writes=[lt])
        return lt

    T.op("dve", lambda: dve.memset(V(TMP0, 1), LN_EPS), writes=[vecB])

    def tsl_of(tau):
        return slice(tau * NT, (tau + 1) * NT)

    def tile_loads(tau):
        tsl = tsl_of(tau)
        T.dma("sp", [(hst[:, h0:h0 + 10, :], hs_d[h0:h0 + 10, :, tsl].rearrange("h p t -> p h t")) for h0 in (0, 10)],
              hstB.sem, reads=hsD, writes=[hstB])
        T.dma("sp", [(bvt[:, c0:c0 + 8, :], bv_d[c0:c0 + 8, :, tsl].rearrange("c p t -> p c t")) for c0 in (0, 8)],
              bvtB.sem, reads=bvD, writes=[bvtB])

    p3_pending = {}

    def p3_load(m, tau):
        if m >= 16 or (m, tau) in p3_pending:
            return
        tsl = tsl_of(tau)
        sgl, sgc = rot(sgA, "sga"), rot(sgB_, "sgb")
        T.dma("sp", [(sgl.t[:, :], g_d[m, :, tsl])], sgl.sem, writes=[sgl])
        T.dma("sp", [(sgc.t[:, :], g_d[16 + m, :, tsl])], sgc.sem, writes=[sgc])
        p3_pending[(m, tau)] = (sgl, sgc)

    def p3(ms, tau):
        for m in ms:
            p3_load(m, tau)
            sgl, sgc = p3_pending.pop((m, tau))
            p3_load(m + 1, tau)
            ta, tb = rot(t1A, "ta"), rot(t2A, "tb")
            T.op("act", lambda: act.activation(out=sgl.t[:, :], in_=sgl.t[:, :], func=AF.Sigmoid), writes=[sgl])
            T.op("act", lambda: act.activation(out=sgc.t[:, :], in_=sgc.t[:, :], func=AF.Sigmoid), writes=[sgc])
            wyl = wload(lambda t, m=m: [(t[:, 0:1280], wlo_d[m, 0, :, :]), (t[:, 1280:2560], wlo_d[m, 1, :, :])], key=("yl", m))
            bk = next_bank()
            T.mm(bk.t[:, :], [(wyl.t[:, k * 128:(k + 1) * 128], hst[:, k, :]) for k in range(H)], reads=[wyl, hstB], writes=[bk])
            T.op("dve", lambda bk=bk: dve.tensor_tensor(out=ta.t[:, :], in0=bk.t[:, :], in1=sgl.t[:, :], op=ALU.mult),
                 reads=[bk, sgl], writes=[ta])
            wyc = wload(lambda t, m=m: [(t[:, 0:2048], wco_d[m, :, :])], key=("yc", m))
            bk = next_bank()
            T.mm(bk.t[:, :], [(wyc.t[:, k * 128:(k + 1) * 128], bvt[:, k, :]) for k in range(16)], reads=[wyc, bvtB], writes=[bk])
            T.op("dve", lambda bk=bk: dve.tensor_tensor(out=tb.t[:, :], in0=bk.t[:, :], in1=sgc.t[:, :], op=ALU.mult),
                 reads=[bk, sgc], writes=[tb])
            T.op("dve", lambda m=m: dve.tensor_tensor(out=mg[:, m, :], in0=ta.t[:, :], in1=tb.t[:, :], op=ALU.add),
                 reads=[ta, tb], writes=[mgB[m]])

    def p4(tau):
        tsl = tsl_of(tau)
        for e in range(16):
            wb = wload(lambda t, e=e: [(t[:, 0:2048], wo_d[e, :, :])])
            xf = rot(xfA, "xf")
            T.dma("sp", [(xf.t[:, :], xT_d[e * 128:(e + 1) * 128, tsl])], xf.sem, writes=[xf])
            bk = next_bank()
            T.mm(bk.t[:, :], [(wb.t[:, k * 128:(k + 1) * 128], mg[:, k, :]) for k in range(16)], reads=[wb] + mgB, writes=[bk])
            T.op("dve", lambda e=e, bk=bk, xf=xf: dve.scalar_tensor_tensor(out=h1[:, e, :], in0=xf.t[:, :], scalar=ALPHA,
                                                                          in1=bk.t[:, :], op0=ALU.mult, op1=ALU.add),
                 reads=[xf, bk], writes=[h1B[e]], extra=([(osem, 16 * 16 * tau)] if tau > 0 else []))
            if e >= LAG:
                ln_stat_chunk(h1, h1B, e - LAG)
        for e in range(16 - LAG, 16):
            ln_stat_chunk(h1, h1B, e)
        ln_finish()

    def ln1_apply(tau, nxt):
        for e in range(16):
            lt = ln_apply_pre(h1, h1B, e, LN1G)
            T.op("act", lambda e=e, lt=lt: act.activation(out=h1[:, e, :], in_=lt.t[:, :], func=AF.Identity, bias=V(LN1B + e)),
                 reads=[lt, vecB], writes=[h1B[e]])
            T.op("act", lambda e=e, lt=lt: act.activation(out=h1bf[:, e, :], in_=lt.t[:, :], func=AF.Identity, bias=V(LN1B + e)),
                 reads=[lt, vecB], writes=[h1bfB[e]])
            if nxt and e % 2 == 1:
                p3([e // 2], tau + 1)

    def ff_group(g):
        fb = g % 2
        for j in range(FG):
            f = g * FG + j
            wb = wload(lambda t, f=f: [(t[:, 0:2048], w1_d[f, :, :])])
            bk = next_bank()
            T.mm(bk.t[:, :], [(wb.t[:, k * 128:(k + 1) * 128], h1bf[:, k, :]) for k in range(16)], reads=[wb] + h1bfB, writes=[bk])
            rl = rot(rlA, "rl")
            T.op("act", lambda bk=bk, rl=rl, f=f: act.activation(out=rl.t[:, :], in_=bk.t[:, :], func=AF.Relu, bias=V(B1 + f)),
                 reads=[bk, vecB], writes=[rl])
            T.op("dve", lambda rl=rl, fb=fb, j=j: dve.tensor_tensor(out=ff[fb][:, j, :], in0=rl.t[:, :], in1=rl.t[:, :], op=ALU.mult),
                 reads=[rl], writes=[ffB[fb][j]])

    def out_group(g):
        fb = g % 2
        for e2 in range(8):
            wb = wload(lambda t, e2=e2, g=g: [(t[:, 0:2 * FG * 128].rearrange("p (m j) -> p m j", m=2),
                                               w2_d[g, 2 * e2:2 * e2 + 2, :, :].rearrange("m p j -> p m j"))])
            for q in range(2):
                e = 2 * e2 + q
                bk = next_bank()
                T.mm(bk.t[:, :], [(wb.t[:, q * FG * 128 + k * 128:q * FG * 128 + (k + 1) * 128], ff[fb][:, k, :]) for k in range(FG)],
                     reads=[wb] + ffB[fb], writes=[bk])
                if g == 0:
                    T.op("dve", lambda e=e, bk=bk: dve.scalar_tensor_tensor(out=h1[:, e, :], in0=h1[:, e, :], scalar=ALPHA, in1=bk.t[:, :],
                                                                          op0=ALU.mult, op1=ALU.add), reads=[bk], writes=[h1B[e]])
                elif g < NFG - 1:
                    T.op("dve", lambda e=e, bk=bk: dve.tensor_tensor(out=h1[:, e, :], in0=h1[:, e, :], in1=bk.t[:, :], op=ALU.add),
                         reads=[bk], writes=[h1B[e]])
                else:
                    T.op("dve", lambda e=e, bk=bk: dve.scalar_tensor_tensor(out=h1[:, e, :], in0=bk.t[:, :], scalar=V(B2 + e), in1=h1[:, e, :],
                                                                          op0=ALU.add, op1=ALU.add), reads=[bk, vecB], writes=[h1B[e]])
                    if e >= LAG:
                        ln_stat_chunk(h1, h1B, e - LAG)
        if g == NFG - 1:
            for e in range(16 - LAG, 16):
                ln_stat_chunk(h1, h1B, e)
            ln_finish()

    def p5():
        ff_group(0)
        for g in range(NFG):
            if g + 1 < NFG:
                ff_group(g + 1)
            out_group(g)

    def ln2_apply_store(tau, nxt):
        tsl = tsl_of(tau)
        for e in range(16):
            lt = ln_apply_pre(h1, h1B, e, LN2G)
            T.op("act", lambda e=e, lt=lt: act.activation(out=h1[:, e, :], in_=lt.t[:, :], func=AF.Identity, bias=V(LN2B + e)),
                 reads=[lt, vecB], writes=[h1B[e]])
            T.dma("sp", [(outT_d[e * 128:(e + 1) * 128, tsl], h1[:, e, :])], osem, reads=[h1B[e]])
            if nxt and e % 2 == 1:
                p3([8 + e // 2], tau + 1)

    if ntiles > 0:
        tile_loads(0)
        p3(range(16), 0)
    for tau in range(ntiles):
        nxt = tau + 1 < ntiles
        p4(tau)
        if nxt:
            tile_loads(tau + 1)
            p3_load(0, tau + 1)
        ln1_apply(tau, nxt)
        p5()
        ln2_apply_store(tau, nxt)

    T.barrier()
    return nc


def _slabify(Wm, G):
    K, M = Wm.shape
    a = Wm.reshape(K // (128 * G), G, 128, M // 128, 128)
    a = a.transpose(3, 0, 2, 1, 4)
    return np.ascontiguousarray(a).reshape(M // 128, K // (128 * G), 128, G * 128)


def _prep_weights(inp):
    f = lambda k: np.asarray(inp[k], dtype=np.float32)
    w_in_s = _slabify(f("w_in"), 16).reshape(100, 128, 2048)
    w_lo_s = _slabify(f("w_lru_out"), 10)
    w_co_s = _slabify(f("w_conv_out"), 16).reshape(16, 128, 2048)
    w_o_s = _slabify(f("w_o"), 16).reshape(16, 128, 2048)
    w1_s = _slabify(f("mlp_w1"), 16).reshape(64, 128, 2048)
    w2_s = np.ascontiguousarray(_slabify(f("mlp_w2"), FG).transpose(1, 0, 2, 3))
    wa, wx = f("lru_w_a"), f("lru_w_x")
    g = np.stack([wa[0], wx[0], wa[1], wx[1]], axis=2)
    w_g_s = np.ascontiguousarray(g).reshape(H, 128, 512)
    vecs = np.zeros((128, NV_IN), np.float32)
    vecs[:, LCW:LCW + H * 4] = f("lru_conv_w").reshape(4, H, 128).transpose(2, 1, 0).reshape(128, H * 4)
    vecs[:, LCB:LCB + H] = f("lru_conv_b").reshape(H, 128).T
    vecs[:, BA:BA + 2 * H] = f("lru_b_a").reshape(2 * H, 128).T
    vecs[:, BX:BX + 2 * H] = f("lru_b_x").reshape(2 * H, 128).T
    vecs[:, LAM:LAM + 2 * H] = f("lru_lambda").reshape(2 * H, 128).T
    vecs[:, SCW:SCW + 48] = f("sc_conv_w").reshape(3, 16, 128).transpose(2, 1, 0).reshape(128, 48)
    vecs[:, SCB:SCB + 16] = f("sc_conv_b").reshape(16, 128).T
    vecs[:, LN1G:LN1G + 16] = f("ln1_g").reshape(16, 128).T
    vecs[:, LN1B:LN1B + 16] = f("ln1_b").reshape(16, 128).T
    vecs[:, LN2G:LN2G + 16] = f("ln2_g").reshape(16, 128).T
    vecs[:, LN2B:LN2B + 16] = f("ln2_b").reshape(16, 128).T
    vecs[:, B2:B2 + 16] = f("mlp_b2").reshape(16, 128).T
    vecs[:, B1:B1 + 64] = f("mlp_b1").reshape(64, 128).T
    return {"w_in_s": w_in_s, "w_g_s": w_g_s, "w_lo_s": w_lo_s, "w_co_s": w_co_s, "w_o_s": w_o_s,
            "w1_s": w1_s, "w2_s": w2_s, "vecs": vecs}


def kernel(**inputs):
    x = np.asarray(inputs["x"], dtype=np.float32)
    B = x.shape[0]
    wts = _prep_weights(inputs)
    nc = build_nc()
    in_maps = []
    for b in range(B):
        m = dict(wts)
        m["xT"] = np.ascontiguousarray(x[b].T)
        in_maps.append(m)
    res = run_bass_kernel_spmd(nc, in_maps, core_ids=list(range(B)))
    out = np.stack([np.ascontiguousarray(np.asarray(r["outT"]).T) for r in res.results], axis=0)
    return out.astype(np.float32, copy=False)
```
